# Optimizing a Trainium2 kernel written in Bass

```python
import jax, jax.numpy as jnp
from jax import lax
import numpy as np

D_MODEL = 4096
BATCH = 1
SEQ = 8192
DEPTH = 4

MEM_LEN = 256
EPS = 1e-6
CONV_WIDTH = D_MODEL // 2
CONV_KERNEL = 31
GLA_HEADS = 4
GLA_KEY = D_MODEL // 4
GLA_VALUE = D_MODEL // 2
GLA_KEY_HEAD = GLA_KEY // GLA_HEADS
GLA_VALUE_HEAD = GLA_VALUE // GLA_HEADS
GLA_LOWRANK = 16
GLA_TAU = 16.0
GLA_CHUNK = 64
HGRN_EXPAND = 128
HGRN_WIDTH = D_MODEL // 2
HGRN_HEADS = HGRN_WIDTH // HGRN_EXPAND
HGRN_CHUNK = 32
N_BRANCH = 3
BRANCH_WIDTH = D_MODEL // 2
XATTN_HEADS = 4
XATTN_DIM = D_MODEL // 4
XATTN_HEAD_DIM = XATTN_DIM // XATTN_HEADS
IN_SPLITS = (CONV_WIDTH, CONV_WIDTH, CONV_WIDTH,
             GLA_KEY, GLA_KEY, GLA_VALUE, GLA_VALUE, GLA_LOWRANK,
             HGRN_WIDTH, HGRN_WIDTH, HGRN_WIDTH, HGRN_WIDTH,
             N_BRANCH * D_MODEL)
N_IN = 3 * CONV_WIDTH + 2 * GLA_KEY + 2 * GLA_VALUE + GLA_LOWRANK + 4 * HGRN_WIDTH + N_BRANCH * D_MODEL

kernel_name = "hybrid_conv_gla_hgrn2_gated_merge"


def rmsnorm(x, g):
    xf = x.astype(jnp.float32)
    y = xf * lax.rsqrt(jnp.mean(xf * xf, axis=-1, keepdims=True) + EPS)
    return (y * g.astype(jnp.float32)).astype(x.dtype)


def layernorm(x, g, b):
    xf = x.astype(jnp.float32)
    mu = jnp.mean(xf, axis=-1, keepdims=True)
    var = jnp.mean(jnp.square(xf - mu), axis=-1, keepdims=True)
    y = (xf - mu) * lax.rsqrt(var + EPS)
    return (y * g.astype(jnp.float32) + b.astype(jnp.float32)).astype(x.dtype)


def split_columns(z):
    offsets = np.cumsum(np.array(IN_SPLITS))[:-1].tolist()
    return jnp.split(z, offsets, axis=-1)


def chunked_gated_linear_attention(q, k, v, log_decay, chunk):
    bsz, seq, heads, dk = q.shape
    dv = v.shape[-1]
    n = seq // chunk

    def to_chunks(t):
        return t.astype(jnp.float32).reshape(bsz, n, chunk, heads, t.shape[-1]).transpose(1, 0, 3, 2, 4)

    q, k, v, g = to_chunks(q), to_chunks(k), to_chunks(v), to_chunks(log_decay)
    b = jnp.cumsum(g, axis=3)
    b_ref = b[:, :, :, chunk // 2 - 1:chunk // 2, :]
    b_last = b[:, :, :, -1:, :]
    causal = jnp.tril(jnp.ones((chunk, chunk), dtype=bool))
    scores = jnp.einsum('nbhid,nbhjd->nbhij', q * jnp.exp(b - b_ref), k * jnp.exp(b_ref - b))
    o_intra = jnp.einsum('nbhij,nbhjv->nbhiv', jnp.where(causal, scores, 0.0), v)
    q_in = q * jnp.exp(b)
    k_out = k * jnp.exp(b_last - b)
    decay = jnp.exp(b_last[:, :, :, 0, :])

    def step(state, xs):
        q_c, k_c, v_c, d_c = xs
        o_c = jnp.einsum('bhid,bhdv->bhiv', q_c, state)
        state = d_c[..., None] * state + jnp.einsum('bhjd,bhjv->bhdv', k_c, v_c)
        return state, o_c

    init = jnp.zeros((bsz, heads, dk, dv), jnp.float32)
    _, o_inter = lax.scan(step, init, (q_in, k_out, v, decay))
    o = o_intra + o_inter
    return o.transpose(1, 0, 3, 2, 4).reshape(bsz, seq, heads, dv)


def conv_branch(c_val, c_glu, c_gate, conv_w, conv_b, ln_g, ln_b, pw):
    a = c_val * jax.nn.sigmoid(c_glu)
    a = lax.conv_general_dilated(a, conv_w[:, None, :], window_strides=(1,),
                                 padding=[(CONV_KERNEL - 1, 0)],
                                 dimension_numbers=('NWC', 'WIO', 'NWC'),
                                 feature_group_count=CONV_WIDTH) + conv_b
    a = jax.nn.silu(layernorm(a, ln_g, ln_b))
    return (a @ pw) * jax.nn.silu(c_gate)


def gla_branch(a_q, a_k, a_v, a_gate, a_lr, wa2, ba, norm_g):
    bsz, seq, _ = a_q.shape
    q = a_q.reshape(bsz, seq, GLA_HEADS, GLA_KEY_HEAD) * (GLA_KEY_HEAD ** -0.5)
    k = a_k.reshape(bsz, seq, GLA_HEADS, GLA_KEY_HEAD)
    v = a_v.reshape(bsz, seq, GLA_HEADS, GLA_VALUE_HEAD)
    log_alpha = jax.nn.log_sigmoid((a_lr @ wa2 + ba).astype(jnp.float32)) / GLA_TAU
    log_alpha = log_alpha.reshape(bsz, seq, GLA_HEADS, GLA_KEY_HEAD)
    o = chunked_gated_linear_attention(q, k, v, log_alpha, GLA_CHUNK)
    o = rmsnorm(o, norm_g).astype(a_v.dtype).reshape(bsz, seq, GLA_VALUE)
    return o * jax.nn.silu(a_gate)


def hgrn_branch(r_q, r_f, r_i, r_gate, lower_bound, norm_g):
    bsz, seq, _ = r_q.shape
    q = (jax.nn.silu(r_q.astype(jnp.float32)) * (HGRN_EXPAND ** -0.5)).reshape(bsz, seq, HGRN_HEADS, HGRN_EXPAND)
    f = lower_bound + (1.0 - lower_bound) * jax.nn.sigmoid(r_f.astype(jnp.float32))
    k = (1.0 - f).reshape(bsz, seq, HGRN_HEADS, HGRN_EXPAND)
    log_f = jnp.log(f).reshape(bsz, seq, HGRN_HEADS, HGRN_EXPAND)
    i = r_i.reshape(bsz, seq, HGRN_HEADS, HGRN_EXPAND)
    o = chunked_gated_linear_attention(q, k, i, log_f, HGRN_CHUNK)
    o = rmsnorm(o, norm_g).astype(r_i.dtype).reshape(bsz, seq, HGRN_WIDTH)
    return o * jax.nn.silu(r_gate)


def cross_attention(hn, memn, wq, wkv, wo):
    bsz, seq, _ = hn.shape
    mlen = memn.shape[1]
    q = (hn @ wq).reshape(bsz, seq, XATTN_HEADS, XATTN_HEAD_DIM)
    k, v = jnp.split(memn @ wkv, 2, axis=-1)
    k = k.reshape(bsz, mlen, XATTN_HEADS, XATTN_HEAD_DIM)
    v = v.reshape(bsz, mlen, XATTN_HEADS, XATTN_HEAD_DIM)
    s = jnp.einsum('bshd,bmhd->bhsm', q.astype(jnp.float32), k.astype(jnp.float32)) * (XATTN_HEAD_DIM ** -0.5)
    p = jax.nn.softmax(s, axis=-1)
    o = jnp.einsum('bhsm,bmhd->bshd', p, v.astype(jnp.float32)).astype(hn.dtype)
    return o.reshape(bsz, seq, XATTN_DIM) @ wo


def setup_inputs(seed: int = 0) -> dict:
    key = jax.random.key(seed)
    ks = jax.random.split(key, 24)

    def nrm(k, shape, scale):
        return jax.random.normal(k, shape, jnp.float32) * scale

    def gain(k, shape):
        return 1.0 + 0.02 * jax.random.normal(k, shape, jnp.float32)

    return {
        "x": nrm(ks[0], (BATCH, SEQ, D_MODEL), 1.0),
        "mem": nrm(ks[1], (BATCH, MEM_LEN, D_MODEL), 1.0),
        "norm1_g": gain(ks[2], (DEPTH, D_MODEL)),
        "w_in": nrm(ks[3], (DEPTH, D_MODEL, N_IN), D_MODEL ** -0.5),
        "conv_w": nrm(ks[4], (DEPTH, CONV_KERNEL, CONV_WIDTH), CONV_KERNEL ** -0.5),
        "conv_b": nrm(ks[5], (DEPTH, CONV_WIDTH), 0.01),
        "conv_ln_g": gain(ks[6], (DEPTH, CONV_WIDTH)),
        "conv_ln_b": nrm(ks[7], (DEPTH, CONV_WIDTH), 0.01),
        "conv_pw": nrm(ks[8], (DEPTH, CONV_WIDTH, CONV_WIDTH), CONV_WIDTH ** -0.5),
        "gla_wa2": nrm(ks[9], (DEPTH, GLA_LOWRANK, GLA_KEY), GLA_LOWRANK ** -0.5),
        "gla_ba": nrm(ks[10], (DEPTH, GLA_KEY), 0.01),
        "gla_norm_g": gain(ks[11], (DEPTH, GLA_VALUE_HEAD)),
        "hgrn_lb": nrm(ks[12], (DEPTH, HGRN_WIDTH), 0.1),
        "hgrn_norm_g": gain(ks[13], (DEPTH, HGRN_EXPAND)),
        "w_branch": nrm(ks[14], (DEPTH, N_BRANCH, BRANCH_WIDTH, D_MODEL), BRANCH_WIDTH ** -0.5),
        "w_out": nrm(ks[15], (DEPTH, D_MODEL, D_MODEL), D_MODEL ** -0.5),
        "norm2_g": gain(ks[16], (DEPTH, D_MODEL)),
        "mem_norm_g": gain(ks[17], (D_MODEL,)),
        "xq": nrm(ks[18], (DEPTH, D_MODEL, XATTN_DIM), D_MODEL ** -0.5),
        "xkv": nrm(ks[19], (DEPTH, D_MODEL, 2 * XATTN_DIM), D_MODEL ** -0.5),
        "xo": nrm(ks[20], (DEPTH, XATTN_DIM, D_MODEL), XATTN_DIM ** -0.5),
        "final_norm_g": gain(ks[21], (D_MODEL,)),
    }


def reference(x, mem, norm1_g, w_in, conv_w, conv_b, conv_ln_g, conv_ln_b, conv_pw,
              gla_wa2, gla_ba, gla_norm_g, hgrn_lb, hgrn_norm_g, w_branch, w_out,
              norm2_g, mem_norm_g, xq, xkv, xo, final_norm_g):
    bsz, seq, _ = x.shape
    lb_all = jnp.cumsum(jax.nn.softmax(hgrn_lb.astype(jnp.float32), axis=0), axis=0)
    lb_all = lb_all - lb_all[:1]
    memn = rmsnorm(mem, mem_norm_g)
    h = x
    for l in range(DEPTH):
        xn = rmsnorm(h, norm1_g[l])
        z = xn @ w_in[l]
        (c_val, c_glu, c_gate, a_q, a_k, a_v, a_gate, a_lr,
         r_q, r_f, r_i, r_gate, merge) = split_columns(z)
        y_conv = conv_branch(c_val, c_glu, c_gate, conv_w[l], conv_b[l], conv_ln_g[l], conv_ln_b[l], conv_pw[l])
        y_gla = gla_branch(a_q, a_k, a_v, a_gate, a_lr, gla_wa2[l], gla_ba[l], gla_norm_g[l])
        y_hgrn = hgrn_branch(r_q, r_f, r_i, r_gate, lb_all[l], hgrn_norm_g[l])
        branches = jnp.stack([y_conv, y_gla, y_hgrn], axis=2)
        proj = jnp.einsum('bsnc,ncd->bsnd', branches, w_branch[l])
        gates = jax.nn.sigmoid(merge.reshape(bsz, seq, N_BRANCH, D_MODEL))
        h = h + jnp.einsum('bsnd,bsnd->bsd', gates, proj) @ w_out[l]
        h = h + cross_attention(rmsnorm(h, norm2_g[l]), memn, xq[l], xkv[l], xo[l])
    return rmsnorm(h, final_norm_g)
```

```python
import contextlib
import numpy as np
import ml_dtypes
import concourse.bass as bass
import concourse.mybir as mybir
from concourse.bass_utils import run_bass_kernel_spmd

F32, BF16 = mybir.dt.float32, mybir.dt.bfloat16
AF = mybir.ActivationFunctionType
ALU = mybir.AluOpType
AX = mybir.AxisListType
EPS = 1e-6
import os as _os
SKIP = _os.environ.get('SKIP', '')


class Cfg:
    def __init__(s, D=4096, SEQ=8192, L=1, TB=1024, LT=4):
        s.D, s.SEQ, s.L, s.TB, s.LT = D, SEQ, L, TB, LT
        s.CW = D // 2; s.KQ = D // 4; s.GH = 4; s.GDK = s.KQ // 4; s.GDV = s.CW // 4
        s.HW = D // 2; s.HH = s.HW // 128; s.LR = 16; s.XD = D // 4; s.XH = 4; s.XHD = s.XD // 4
        s.M = 256; s.KC = D // 128; s.NSB = SEQ // TB; s.CK = 31
        c = 0; s.off = {}
        for n, w in [("c_val", s.CW), ("c_glu", s.CW), ("c_gate", s.CW), ("a_q", s.KQ), ("a_k", s.KQ),
                     ("a_v", s.CW), ("a_gate", s.CW), ("a_lr", 16), ("r_q", s.HW), ("r_f", s.HW),
                     ("r_i", s.HW), ("r_gate", s.HW), ("merge", 3 * D)]:
            s.off[n] = c; c += w
        s.NIN = c
        v = 0; s.vo = {}
        def add(n, w):
            nonlocal v
            s.vo[n] = v; v += w
        for l in range(L):
            add(("n1g", l), s.KC); add(("n2g", l), s.KC); add(("cw", l), (s.CW // 128) * 31)
            add(("cb", l), s.CW // 128); add(("lng", l), s.CW // 128); add(("lnb", l), s.CW // 128)
            add(("ba", l), s.KQ // 128); add(("gng", l), s.GDV // 128); add(("hlb", l), s.HW // 128)
            add(("hng", l), 1)
        add("fng", s.KC); add("mng", s.KC)
        for m in range(LT): add(("hlbA", m), s.HW // 128)
        add("cm", LT)
        s.NV = v


def colz(v):
    return np.ascontiguousarray(v.reshape(-1, 128).T)


def pack_vecs(cfg, I, lay):
    V = np.zeros((128, cfg.NV), np.float32)
    def put(key, arr):
        o = cfg.vo[key]; V[:, o:o + arr.shape[1]] = arr
    l = 0
    put(("n1g", l), colz(I["norm1_g"][lay])); put(("n2g", l), colz(I["norm2_g"][lay]))
    cw = I["conv_w"][lay]
    put(("cw", l), np.concatenate([cw[:, f * 128:(f + 1) * 128].T for f in range(cfg.CW // 128)], axis=1))
    put(("cb", l), colz(I["conv_b"][lay])); put(("lng", l), colz(I["conv_ln_g"][lay]))
    put(("lnb", l), colz(I["conv_ln_b"][lay])); put(("ba", l), colz(I["gla_ba"][lay]))
    put(("gng", l), colz(I["gla_norm_g"][lay])); put(("hlb", l), colz(I["hgrn_lb"][lay]))
    put(("hng", l), colz(I["hgrn_norm_g"][lay]))
    put("fng", colz(I["final_norm_g"])); put("mng", colz(I["mem_norm_g"]))
    for m in range(cfg.LT): put(("hlbA", m), colz(I["hgrn_lb"][m]))
    cm = np.zeros((128, cfg.LT), np.float32)
    cm[:, 1:lay + 1] = 1.0
    put("cm", cm)
    return V


def make_consts(cfg):
    TB = cfg.TB
    C = {}
    C["ident"] = np.eye(128, dtype=np.float32).astype(ml_dtypes.bfloat16)
    C["ones"] = np.ones((128, 128), np.float32).astype(ml_dtypes.bfloat16)
    j = np.arange(128)[:, None]; i = np.arange(128)[None, :]
    for nm, cs in (("maskg", 64), ("maskh", 32)):
        C[nm] = (((j // cs) == (i // cs)) & (j <= i)).astype(np.float32)
    t = np.arange(TB)
    C["rstg"] = np.broadcast_to(((t % 64) != 0).astype(np.float32), (128, TB)).astype(ml_dtypes.bfloat16)
    C["rsth"] = np.broadcast_to(((t % 32) != 0).astype(np.float32), (128, TB)).astype(ml_dtypes.bfloat16)
    return C


class Buf:
    __slots__ = ("w", "r")
    def __init__(s):
        s.w = None; s.r = []


class TR:
    ENG = ["pe", "act", "dve", "pool", "sp"]
    def __init__(s):
        s.ops = {e: [] for e in s.ENG}
        s.cnt = {e: 0 for e in s.ENG}
        s.dman = {"sp": 0, "pool": 0}
        s.KD = {"sp": 8, "pool": 2}
        s.dlast = {}
        s.pending = {e: [] for e in s.ENG}
        s.nbar = 0; s.stop_after = None; s.dead = False
    def emit(s, eng, fn, reads=(), writes=(), dma=False):
        if s.dead: return None
        waits = list(s.pending[eng]); s.pending[eng] = []
        for b in reads:
            if b.w is not None: waits.append(b.w)
        for b in writes:
            if b.w is not None: waits.append(b.w)
            waits.extend(b.r)
        if dma:
            n = s.dman[eng]; s.dman[eng] += 1; K = s.KD[eng]; slot = n % K
            key = ("d", eng, slot); val = 16 * (n // K + 1)
            if n >= K: waits.append((key, val - 16))
            ev = (key, val); inc = (key, 16); s.dlast[key] = val
        else:
            s.cnt[eng] += 1; ev = (eng, s.cnt[eng]); inc = (eng, 1)
        for b in reads: b.r.append(ev)
        for b in writes:
            b.w = ev; b.r = []
        s.ops[eng].append((waits, fn, inc))
        return ev
    def barrier(s):
        if s.dead: return
        s.nbar += 1
        if s.stop_after is not None and s.nbar >= s.stop_after: s.dead = True
        allw = [(e, s.cnt[e]) for e in s.ENG if s.cnt[e] > 0] + list(s.dlast.items())
        for e in s.ENG:
            s.pending[e].extend(allw)
    def keys(s):
        ks = set(s.ENG)
        for e in ("sp", "pool"):
            for i in range(s.KD[e]): ks.add(("d", e, i))
        return ks


class Arena:
    def __init__(s, regions):
        s.regions = regions
        s.reset()
    def reset(s):
        s.pos = [0 for _ in s.regions]
    def alloc(s, nelem, dt):
        nb = nelem * (4 if dt == F32 else 2)
        nb_al = (nb + 63) // 64 * 64
        for i, (t, cap) in enumerate(s.regions):
            if s.pos[i] + nb_al <= cap:
                o = s.pos[i]; s.pos[i] += nb_al
                ap = t[:, o // 2:(o + nb) // 2]
                return ap.bitcast(F32) if dt == F32 else ap
        raise RuntimeError("arena full")


def build(cfg, dbg=(), stop_after=None):
    D, SEQ, L, TB = cfg.D, cfg.SEQ, cfg.L, cfg.TB
    CW, KQ, HW, XD, M, KC = cfg.CW, cfg.KQ, cfg.HW, cfg.XD, cfg.M, cfg.KC
    NTB = TB // 512
    nc = bass.Bass("TRN2", target_bir_lowering=False)
    def din(name, shape, dt=F32):
        return nc.dram_tensor(name, list(shape), dt, kind="ExternalInput").ap()
    def dscr(name, shape, dt=F32):
        kind = "ExternalOutput" if (name in dbg or name == "hT_d") else "Internal"
        return nc.dram_tensor(name, list(shape), dt, kind=kind).ap()
    xT = din("xT", [D, SEQ]); memT = din("memT", [D, M])
    w_in = din("w_in", [L, D, cfg.NIN]); conv_pw = din("conv_pw", [L, CW, CW])
    wa2 = din("gla_wa2", [L, 16, KQ]); w_branch = din("w_branch", [L, 3 * CW, D])
    w_out = din("w_out", [L, D, D]); xq = din("xq", [L, D, XD]); xkv = din("xkv", [L, D, 2 * XD])
    xo = din("xo", [L, XD, D]); vecs_d = din("vecs", [128, cfg.NV])
    ident_d = din("ident", [128, 128], BF16); ones_d = din("ones", [128, 128], BF16)
    maskg_d = din("maskg", [128, 128]); maskh_d = din("maskh", [128, 128])
    rstg_d = din("rstg", [128, TB], BF16); rsth_d = din("rsth", [128, TB], BF16)
    outT = nc.dram_tensor("outT", [D, SEQ], F32, kind="ExternalOutput").ap()
    hT_d = dscr("hT_d", [D, SEQ]); aT_d = dscr("aT_d", [CW, 32 + SEQ])
    gate_d = dscr("gate_d", [3, CW, TB]); gq_d = dscr("gq_d", [KQ, TB]); gk_d = dscr("gk_d", [KQ, TB])
    gg_d = dscr("gg_d", [KQ, TB]); gv_d = dscr("gv_d", [TB, CW], BF16); alr_d = dscr("alr_d", [16, TB], BF16)
    hq_d = dscr("hq_d", [HW, TB]); hk_d = dscr("hk_d", [HW, TB]); hg_d = dscr("hg_d", [HW, TB])
    hv_d = dscr("hv_d", [TB, HW], BF16); merge_d = dscr("merge_d", [3, D, TB])
    y_d = dscr("y_d", [3 * CW, TB], BF16); u_d = dscr("u_d", [D, TB], BF16)
    stg_d = dscr("stg_d", [KQ, cfg.GDV]); sth_d = dscr("sth_d", [HW, 128])
    memn_d = dscr("memn_d", [D, M], BF16); kx_d = dscr("kx_d", [XD, M], BF16); vx_d = dscr("vx_d", [M, XD], BF16)
    xn_dbg = dscr("xn_dbg", [D, TB], BF16)
    lb_dbg = dscr("lb_dbg", [128, 64])
    qx_dbg = dscr("qx_dbg", [XD, TB], BF16); ox_dbg = dscr("ox_dbg", [XD, TB], BF16)
    AOFF = 32

    tr = TR(); tr.stop_after = stop_after
    es = contextlib.ExitStack()
    def sb(name, shape, dt):
        return es.enter_context(nc.sbuf_tensor(name, list(shape), dt))
    bigA = sb("bigA", [128, 32768], BF16); bigB = sb("bigB", [128, 16384], BF16)
    NWS = 2
    wbuf = [sb(f"wbuf{i}", [128, 8192], BF16) for i in range(NWS)]; wB = [Buf() for _ in range(NWS)]
    NSTG = 6
    stg = [sb(f"stg{i}", [128, 512], F32) for i in range(NSTG)]; stgB = [Buf() for _ in range(NSTG)]
    NBS = 4
    bst = [sb(f"bst{i}", [128, 512], BF16) for i in range(NBS)]; bstB = [Buf() for _ in range(NBS)]
    keep = [sb(f"keep{i}", [128, 512], F32) for i in range(2)]; keepB = [Buf(), Buf()]
    vecs = sb("vecs_sb", [128, cfg.NV], F32); vecsB = Buf()
    ident = sb("ident_sb", [128, 128], BF16); ones = sb("ones_sb", [128, 128], BF16)
    maskg = sb("maskg_sb", [128, 128], F32); maskh = sb("maskh_sb", [128, 128], F32)
    rstg = sb("rstg_sb", [128, TB], BF16); rsth = sb("rsth_sb", [128, TB], BF16)
    lbv = sb("lbv", [128, HW // 128], F32); omlb = sb("omlb", [128, HW // 128], F32)
    negba = sb("negba", [128, KQ // 128], F32)
    smalls = sb("smalls", [128, 64], F32); smallB = Buf()
    aexts = [sb(f"aext{i}", [128, TB + 32], F32) for i in range(2)]; aextB = [Buf(), Buf()]
    constB = Buf(); lbB = Buf()
    ps = [es.enter_context(nc.psum_tensor(f"ps{i}", [128, 512], F32)) for i in range(7)]
    psB = [Buf() for _ in range(7)]
    pT = es.enter_context(nc.psum_tensor("pT", [128, 1024], BF16)); pTB = Buf()
    arena = Arena([(bigA, 65536), (bigB, 32768)])
    st = {"w": 0, "s": 0, "b": 0, "p": 0}
    def nstg():
        i = st["s"] % NSTG; st["s"] += 1; return stg[i], stgB[i]
    def nbst():
        i = st["b"] % NBS; st["b"] += 1; return bst[i], bstB[i]
    def npsum(lo=0, hi=7):
        i = lo + st["p"] % (hi - lo); st["p"] += 1; return ps[i], psB[i]
    def vcol(key, i=0, n=1):
        o = cfg.vo[key] + i
        return vecs[:, o:o + n]
    def load(out, in_, wr, rd=()):
        return tr.emit("sp", lambda e: e.dma_start(out=out, in_=in_), reads=list(rd), writes=list(wr), dma=True)
    store = lambda out, in_, rd: tr.emit("sp", lambda e: e.dma_start(out=out, in_=in_), reads=list(rd), dma=True)
    def act(out, in_, func, rd, wr, bias=0.0, scale=1.0, accum=None):
        def f(e):
            kw = {}
            if accum is not None: kw["accum_out"] = accum
            return e.activation(out=out, in_=in_, func=func, bias=bias, scale=scale, **kw)
        return tr.emit("act", f, reads=list(rd), writes=list(wr))
    def dve(fn, rd, wr):
        return tr.emit("dve", fn, reads=list(rd), writes=list(wr))

    load(vecs[:], vecs_d[:, :], [vecsB]); load(ident[:], ident_d[:, :], [constB]); load(ones[:], ones_d[:, :], [constB])
    load(maskg[:], maskg_d[:, :], [constB]); load(maskh[:], maskh_d[:, :], [constB])
    load(rstg[:], rstg_d[:, :], [constB]); load(rsth[:], rsth_d[:, :], [constB])
    zt, ztB = nstg()
    dve(lambda e: e.memset(zt[:, 0:32], 0.0), [], [ztB])
    for f in range(CW // 128):
        store(aT_d[f * 128:(f + 1) * 128, 0:AOFF], zt[:, 0:AOFF], [ztB])
    tr.barrier()

    def wload(W, Kc, c0, cb):
        i = st["w"] % NWS; st["w"] += 1
        wt = wbuf[i][:, 0:Kc * cb].rearrange("p (c n) -> p c n", n=cb)
        src = W[:, c0:c0 + cb].rearrange("(c p) n -> p c n", p=128) if Kc * 128 == W.shape[0] else None
        if src is None:
            kr = W.shape[0]
            wt2 = wbuf[i][0:kr, 0:cb]
            wt = wbuf[i][:, 0:cb].rearrange("p (c n) -> p c n", c=1)
            wfull = wbuf[i][:, 0:cb]
            tr.emit("dve", lambda e: e.memset(wfull, 0.0), writes=[wB[i]])
            tr.emit("pool", lambda e: e.dma_start(out=wt2, in_=W[:, c0:c0 + cb]), writes=[wB[i]], dma=True)
        else:
            tr.emit("pool", lambda e: e.dma_start(out=wt, in_=src), writes=[wB[i]], dma=True)
        return wt, wB[i]

    def gemm_fm(W, Kc, c0, ncols, rhs, rhsB, ntb, nfree, epi, cbmax=256, groups=1):
        cbmax = min(cbmax, 8192 // Kc)
        for cc in range(c0, c0 + ncols, cbmax):
            cb = min(cbmax, c0 + ncols - cc)
            wt, wtB = wload(W, Kc, cc, cb)
            for j0 in range(0, cb, 128):
                m = min(128, cb - j0)
                for tb in range(ntb):
                    pss = [npsum() for _ in range(groups)]
                    kg = Kc // groups
                    for g in range(groups):
                        pt_, ptB = pss[g]
                        def mm(e, g=g, pt_=pt_, j0=j0, m=m, tb=tb, wt=wt):
                            for k in range(kg):
                                ins = e.matmul(pt_[0:m, 0:nfree], lhsT=wt[:, g * kg + k, j0:j0 + m], rhs=rhs(g * kg + k, tb),
                                               start=(k == 0), stop=(k == kg - 1))
                            return ins
                        tr.emit("pe", mm, reads=[wtB] + list(rhsB), writes=[ptB])
                    epi(cc + j0, m, tb, [p[0] for p in pss], [p[1] for p in pss])

    def gemm_tm(W, Kc, c0, ncols, lhs, lhsB, ntt, epi, cbmax=256):
        for cc in range(c0, c0 + ncols, cbmax):
            cb = min(cbmax, c0 + ncols - cc)
            wt, wtB = wload(W, Kc, cc, cb)
            for tt in range(ntt):
                pt_, ptB = npsum()
                def mm(e, pt_=pt_, tt=tt, wt=wt, cb=cb):
                    for k in range(Kc):
                        ins = e.matmul(pt_[:, 0:cb], lhsT=lhs(k, tt), rhs=wt[:, k, 0:cb], start=(k == 0), stop=(k == Kc - 1))
                    return ins
                tr.emit("pe", mm, reads=[wtB] + list(lhsB), writes=[ptB])
                epi(cc, cb, tt, pt_, ptB)

    def norm(src, ntok, gkey, out_sb=None, out_sbB=None, out_d=None, out_dt=BF16):
        bw = min(512, ntok)
        for tb in range(ntok // bw):
            sl = slice(tb * bw, (tb + 1) * bw)
            pss, pssB = npsum()
            for k in range(KC):
                ht, htB = nstg()
                load(ht[:, 0:bw], src[k * 128:(k + 1) * 128, sl], [htB])
                sq, sqB = nbst()
                act(sq[:, 0:bw], ht[:, 0:bw], AF.Square, [htB], [sqB])
                tr.emit("pe", lambda e, k=k, sq=sq, pss=pss: e.matmul(pss[:, 0:bw], lhsT=ones[:], rhs=sq[:, 0:bw], start=(k == 0), stop=(k == KC - 1)),
                        reads=[sqB, constB], writes=[pssB])
            rs, rsB = keep[0], keepB[0]
            act(rs[:, 0:bw], pss[:, 0:bw], AF.Sqrt, [pssB], [rsB], bias=EPS, scale=1.0 / D)
            dve(lambda e: e.reciprocal(out=rs[:, 0:bw], in_=rs[:, 0:bw]), [rsB], [rsB])
            for k in range(KC):
                ht, htB = nstg()
                load(ht[:, 0:bw], src[k * 128:(k + 1) * 128, sl], [htB])
                if out_sb is not None:
                    o = out_sb[:, k, sl]
                    dve(lambda e, ht=ht, o=o, k=k: e.scalar_tensor_tensor(out=o, in0=ht[:, 0:bw], scalar=vcol(gkey, k), in1=rs[:, 0:bw], op0=ALU.mult, op1=ALU.mult),
                        [htB, rsB, vecsB], [out_sbB])
                else:
                    if out_dt == BF16:
                        ot, otB = nbst()
                    else:
                        ot, otB = nstg()
                    dve(lambda e, ht=ht, ot=ot, k=k: e.scalar_tensor_tensor(out=ot[:, 0:bw], in0=ht[:, 0:bw], scalar=vcol(gkey, k), in1=rs[:, 0:bw], op0=ALU.mult, op1=ALU.mult),
                        [htB, rsB, vecsB], [otB])
                    store(out_d[k * 128:(k + 1) * 128, sl], ot[:, 0:bw], [otB])

    def linattn(H, dk, dv, CS, q_d, k_d, g_d, v_d, gate_ap, gnkey, state_d, y_ap, mask, rst, first):
        NDC, NVC, NCH, NW, CPW = dk // 128, dv // 128, TB // CS, TB // 128, 128 // CS
        for h in range(H):
            arena.reset()
            A = arena.alloc
            tmpq = A(TB, F32); tmpk = A(TB, F32); tmpg = A(TB, F32); tb_ = A(TB, F32); td = A(TB, F32); te = A(TB, F32)
            tB = {n: Buf() for n in ("q", "k", "g", "b", "d", "e")}
            qtil = [A(TB, BF16) for _ in range(NDC)]; ktil = [A(TB, BF16) for _ in range(NDC)]
            qin = [A(TB, BF16) for _ in range(NDC)]; kout = [A(TB, BF16) for _ in range(NDC)]
            prepB = [Buf() for _ in range(NDC)]
            dec = A(NDC * NCH, F32); decB = Buf()
            ktok = A(NCH * dk, BF16).rearrange("p (c d) -> p c d", d=dk); ktokB = Buf()
            vwin = A(NW * dv, BF16).rearrange("p (w v) -> p w v", v=dv); vchk = A(NCH * dv, BF16).rearrange("p (c v) -> p c v", v=dv)
            vB = Buf()
            Sf = A(NDC * dv, F32).rearrange("p (c v) -> p c v", v=dv); Sb = A(NDC * dv, BF16).rearrange("p (c v) -> p c v", v=dv)
            SfB = Buf(); SbB = [Buf() for _ in range(NDC)]
            of = A(NVC * TB, F32).rearrange("p (c t) -> p c t", t=TB); ofB = Buf()
            sT = [A(128, BF16), A(128, BF16)]; sTB = [Buf(), Buf()]
            rows = slice(h * dk, (h + 1) * dk)
            if first:
                dve(lambda e: e.memset(Sf.rearrange("p c v -> p (c v)"), 0.0), [], [SfB])
            else:
                load(Sf, state_d[rows, :].rearrange("(c p) v -> p c v", p=128), [SfB])
            for dc in range(NDC):
                act(Sb[:, dc, :], Sf[:, dc, :], AF.Copy, [SfB], [SbB[dc]])
            dve(lambda e: e.memset(ktok.rearrange("p c d -> p (c d)"), 0.0), [], [ktokB])
            dve(lambda e: e.memset(vchk.rearrange("p c v -> p (c v)"), 0.0), [], [vB])
            load(vwin, v_d[:, h * dv:(h + 1) * dv].rearrange("(w p) v -> p w v", p=128), [vB])
            load(vchk[0:CS], v_d[:, h * dv:(h + 1) * dv].rearrange("(c p) v -> p c v", p=CS), [vB])
            tr.barrier()
            for dc in range(NDC):
                r = slice(h * dk + dc * 128, h * dk + (dc + 1) * 128)
                load(tmpq, q_d[r, :], [tB["q"]]); load(tmpk, k_d[r, :], [tB["k"]]); load(tmpg, g_d[r, :], [tB["g"]])
                dve(lambda e: e.tensor_tensor_scan(out=tb_, data0=rst[:], data1=tmpg, initial=0.0, op0=ALU.mult, op1=ALU.add),
                    [tB["g"], constB], [tB["b"]])
                b3 = tb_.rearrange("p (c s) -> p c s", s=CS); d3 = td.rearrange("p (c s) -> p c s", s=CS)
                bref = b3[:, :, CS // 2 - 1:CS // 2].to_broadcast([128, NCH, CS])
                blast = b3[:, :, CS - 1:CS].to_broadcast([128, NCH, CS])
                pB = prepB[dc]
                dve(lambda e, b3=b3, d3=d3, bref=bref: e.tensor_tensor(out=d3, in0=b3, in1=bref, op=ALU.subtract), [tB["b"]], [tB["d"]])
                act(te, td, AF.Exp, [tB["d"]], [tB["e"]])
                dve(lambda e, dc=dc: e.tensor_tensor(out=qtil[dc], in0=tmpq, in1=te, op=ALU.mult), [tB["q"], tB["e"]], [pB])
                act(te, td, AF.Exp, [tB["d"]], [tB["e"]], scale=-1.0)
                dve(lambda e, dc=dc: e.tensor_tensor(out=ktil[dc], in0=tmpk, in1=te, op=ALU.mult), [tB["k"], tB["e"]], [pB])
                act(te, tb_, AF.Exp, [tB["b"]], [tB["e"]])
                dve(lambda e, dc=dc: e.tensor_tensor(out=qin[dc], in0=tmpq, in1=te, op=ALU.mult), [tB["q"], tB["e"]], [pB])
                dve(lambda e, b3=b3, d3=d3, blast=blast: e.tensor_tensor(out=d3, in0=blast, in1=b3, op=ALU.subtract), [tB["b"]], [tB["d"]])
                act(te, td, AF.Exp, [tB["d"]], [tB["e"]])
                dve(lambda e, dc=dc: e.tensor_tensor(out=kout[dc], in0=tmpk, in1=te, op=ALU.mult), [tB["k"], tB["e"]], [pB])
                dsl = dec[:, dc * NCH:(dc + 1) * NCH]
                act(dsl.rearrange("p (c o) -> p c o", o=1), b3[:, :, CS - 1:CS], AF.Exp, [tB["b"]], [decB])
                tr.barrier()
                for c0 in range(0, NCH, 8):
                    def tp(e, dc=dc, c0=c0):
                        for c in range(c0, c0 + 8):
                            ins = e.transpose(out=pT[0:CS, (c - c0) * 128:(c - c0 + 1) * 128], in_=kout[dc][:, c * CS:(c + 1) * CS], identity=ident[:])
                        return ins
                    tr.emit("pe", tp, reads=[pB, constB], writes=[pTB])
                    dve(lambda e, dc=dc, c0=c0: e.tensor_copy(out=ktok[0:CS, c0:c0 + 8, dc * 128:(dc + 1) * 128],
                                                              in_=pT[0:CS, :].rearrange("p (c d) -> p c d", d=128)), [pTB], [ktokB])
            tr.barrier()
            oB = [[Buf() for _ in range(4)] for _ in range(NVC)]
            for w in range(NW):
                wsl = slice(w * 128, (w + 1) * 128)
                sc, scB = ps[0], psB[0]
                def m1(e, wsl=wsl):
                    for dc in range(NDC):
                        ins = e.matmul(sc[:, 0:128], lhsT=ktil[dc][:, wsl], rhs=qtil[dc][:, wsl], start=(dc == 0), stop=(dc == NDC - 1))
                    return ins
                tr.emit("pe", m1, reads=prepB, writes=[scB])
                s_t, s_tB = sT[w % 2], sTB[w % 2]
                dve(lambda e, s_t=s_t: e.tensor_tensor(out=s_t, in0=sc[:, 0:128], in1=mask[:], op=ALU.mult), [scB, constB], [s_tB])
                col = (w % 4) * 128
                for vc in range(NVC):
                    tr.emit("pe", lambda e, vc=vc, col=col, w=w, s_t=s_t: e.matmul(ps[1 + vc][:, col:col + 128], lhsT=vwin[:, w, vc * 128:(vc + 1) * 128], rhs=s_t, start=True, stop=False),
                            reads=[vB, s_tB], writes=[oB[vc][w % 4]])
                for ci in range(CPW):
                    c = w * CPW + ci
                    csl = slice(c * CS, (c + 1) * CS); ccol = col + ci * CS
                    last = (ci == CPW - 1)
                    for vc in range(NVC if "M2" not in SKIP else 0):
                        def m2(e, vc=vc, csl=csl, ccol=ccol, last=last):
                            for dc in range(NDC):
                                ins = e.matmul(ps[1 + vc][:, ccol:ccol + CS], lhsT=Sb[:, dc, vc * 128:(vc + 1) * 128], rhs=qin[dc][:, csl],
                                               start=False, stop=(last and dc == NDC - 1), skip_group_check=True)
                            return ins
                        tr.emit("pe", m2, reads=SbB + prepB, writes=[oB[vc][w % 4]])
                    for dc in range(NDC if "U" not in SKIP else 0):
                        ub, ubB = ps[5 + dc], psB[5 + dc]
                        tr.emit("pe", lambda e, dc=dc, c=c, ub=ub: e.matmul(ub[:, 0:dv], lhsT=ktok[:, c, dc * 128:(dc + 1) * 128], rhs=vchk[:, c, :], start=True, stop=True),
                                reads=[ktokB, vB], writes=[ubB])
                        if "UPD" in SKIP: continue
                        def upd(e, dc=dc, c=c, ub=ub):
                            e.tensor_scalar(out=Sf[:, dc, :], in0=Sf[:, dc, :], scalar1=dec[:, dc * NCH + c:dc * NCH + c + 1], scalar2=None, op0=ALU.mult)
                            return e.tensor_tensor(out=Sf[:, dc, :], in0=Sf[:, dc, :], in1=ub[:, 0:dv], op=ALU.add)
                        dve(upd, [ubB, decB, SfB], [SfB])
                        dve(lambda e, dc=dc: e.tensor_copy(out=Sb[:, dc, :], in_=Sf[:, dc, :]), [SfB], [SbB[dc]])
                for vc in range(NVC):
                    act(of[:, vc, wsl], ps[1 + vc][:, col:col + 128], AF.Copy, [oB[vc][w % 4]], [ofB])
            tr.barrier()
            if state_d is not None:
                store(state_d[rows, :].rearrange("(c p) v -> p c v", p=128), Sf, [SfB])
            for tb in range(NTB):
                sl = slice(tb * 512, (tb + 1) * 512)
                pss, pssB = ps[0], psB[0]
                for vc in range(NVC):
                    sq, sqB = nbst()
                    act(sq[:], of[:, vc, sl], AF.Square, [ofB], [sqB])
                    tr.emit("pe", lambda e, vc=vc, sq=sq: e.matmul(pss[:, :], lhsT=ones[:], rhs=sq[:], start=(vc == 0), stop=(vc == NVC - 1)), reads=[sqB, constB], writes=[pssB])
                rs, rsB = keep[0], keepB[0]
                act(rs[:], pss[:, :], AF.Sqrt, [pssB], [rsB], bias=EPS, scale=1.0 / dv)
                dve(lambda e, rs=rs: e.reciprocal(out=rs[:], in_=rs[:]), [rsB], [rsB])
                for vc in range(NVC):
                    gt, gtB = nstg()
                    grow = slice(h * dv + vc * 128, h * dv + (vc + 1) * 128)
                    load(gt[:], gate_ap[grow, sl], [gtB])
                    t1, t1B = nstg()
                    dve(lambda e, vc=vc, t1=t1, rs=rs, sl=sl: e.scalar_tensor_tensor(out=t1[:], in0=of[:, vc, sl], scalar=vcol(gnkey, vc if NVC > 1 else 0), in1=rs[:], op0=ALU.mult, op1=ALU.mult),
                        [ofB, rsB, vecsB], [t1B])
                    yb, ybB = nbst()
                    dve(lambda e, t1=t1, gt=gt, yb=yb: e.tensor_tensor(out=yb[:], in0=t1[:], in1=gt[:], op=ALU.mult), [t1B, gtB], [ybB])
                    store(y_ap[grow, sl], yb[:], [ybB])
            tr.barrier()

    norm(memT, M, "mng", out_d=memn_d)
    tr.barrier()

    for l in range(L):
        Wl = w_in[l]
        HC = HW // 128
        sm = smalls
        LT = cfg.LT
        chB = Buf()
        def step(fn):
            dve(fn, [chB, vecsB] + stgB[:LT], [chB, smallB, lbB] + stgB[:LT])
        mx = sm[:, 0:HC]; ssum = sm[:, 16:16 + HC]
        step(lambda e: e.tensor_copy(out=mx, in_=vcol(("hlbA", 0), 0, HC)))
        for m_ in range(1, LT):
            step(lambda e, m_=m_: e.tensor_tensor(out=mx, in0=mx, in1=vcol(("hlbA", m_), 0, HC), op=ALU.max))
        ex = [stg[i] for i in range(LT)]
        for m_ in range(LT):
            step(lambda e, m_=m_: e.tensor_tensor(out=ex[m_][:, 0:HC], in0=vcol(("hlbA", m_), 0, HC), in1=mx, op=ALU.subtract))
            act(ex[m_][:, 0:HC], ex[m_][:, 0:HC], AF.Exp, [chB, stgB[m_]], [chB, stgB[m_]])
        step(lambda e: e.tensor_copy(out=ssum, in_=ex[0][:, 0:HC]))
        for m_ in range(1, LT):
            step(lambda e, m_=m_: e.tensor_tensor(out=ssum, in0=ssum, in1=ex[m_][:, 0:HC], op=ALU.add))
        step(lambda e: e.reciprocal(out=ssum, in_=ssum))
        step(lambda e: e.memset(lbv[:], 0.0))
        for m_ in range(LT):
            step(lambda e, m_=m_: e.scalar_tensor_tensor(out=lbv[:], in0=ex[m_][:, 0:HC], scalar=vcol("cm", m_), in1=lbv[:], op0=ALU.mult, op1=ALU.add))
        step(lambda e: e.tensor_tensor(out=lbv[:], in0=lbv[:], in1=ssum, op=ALU.mult))
        step(lambda e: e.tensor_scalar(out=omlb[:], in0=lbv[:], scalar1=-1.0, scalar2=1.0, op0=ALU.mult, op1=ALU.add))
        step(lambda e: e.tensor_scalar(out=negba[:], in0=vcol(("ba", l), 0, KQ // 128), scalar1=-1.0, scalar2=None, op0=ALU.mult))
        if "lb_dbg" in dbg:
            store(lb_dbg[:, 0:HC], lbv[:], [lbB]); store(lb_dbg[:, 16:16 + HC], omlb[:], [lbB]); store(lb_dbg[:, 32:64], smalls[:, 0:32], [smallB])
        tr.barrier()
        arena.reset()
        mn = arena.alloc(KC * M, BF16).rearrange("p (c m) -> p c m", m=M); mnB = Buf()
        load(mn, memn_d.rearrange("(c p) m -> p c m", p=128), [mnB])
        def epi_kx(col, m, tb, pp, ppB):
            t, tB_ = nbst()
            act(t[0:m, 0:M], pp[0][0:m, 0:M], AF.Copy, [ppB[0]], [tB_])
            store(kx_d[col:col + m, :], t[0:m, 0:M], [tB_])
        gemm_fm(xkv[l], KC, 0, XD, lambda k, tb: mn[:, k, :], [mnB], 1, M, epi_kx)
        def epi_vx(col, cb, tt, pp, ppB):
            t, tB_ = nbst()
            act(t[:, 0:cb], pp[:, 0:cb], AF.Copy, [ppB], [tB_])
            store(vx_d[tt * 128:(tt + 1) * 128, col - XD:col - XD + cb], t[:, 0:cb], [tB_])
        gemm_tm(xkv[l], KC, XD, XD, lambda k, tt: mn[:, k, tt * 128:(tt + 1) * 128], [mnB], M // 128, epi_vx)
        tr.barrier()

        for sbi in range(cfg.NSB):
            t0 = sbi * TB
            hsrc = xT if l == 0 else hT_d
            arena.reset()
            xn = arena.alloc(KC * TB, BF16).rearrange("p (c t) -> p c t", t=TB); xnB = Buf()
            norm(hsrc[:, t0:t0 + TB], TB, ("n1g", l), out_sb=xn, out_sbB=xnB)
            rhs_xn = lambda k, tb: xn[:, k, tb * 512:(tb + 1) * 512]
            if "xn_dbg" in dbg:
                for k in range(KC):
                    store(xn_dbg[k * 128:(k + 1) * 128, :], xn[:, k, :], [xnB])
            valbuf = arena.alloc(2 * TB, F32).rearrange("p (j t) -> p j t", t=TB); valB = Buf()
            def epi_val(col, m, tb, pp, ppB):
                j = ((col - cfg.off["c_val"]) // 128) % 2
                dve(lambda e: e.tensor_copy(out=valbuf[:, j, tb * 512:(tb + 1) * 512], in_=pp[0][:, :]), [ppB[0]], [valB])
            def epi_glu(col, m, tb, pp, ppB):
                f = (col - cfg.off["c_glu"]) // 128; j = f % 2
                s1, s1B = nstg()
                act(s1[:], pp[0][:, :], AF.Sigmoid, [ppB[0]], [s1B])
                s2, s2B = nstg()
                dve(lambda e: e.tensor_tensor(out=s2[:], in0=valbuf[:, j, tb * 512:(tb + 1) * 512], in1=s1[:], op=ALU.mult), [valB, s1B], [s2B])
                store(aT_d[f * 128:(f + 1) * 128, AOFF + t0 + tb * 512:AOFF + t0 + (tb + 1) * 512], s2[:], [s2B])
            for i in range(CW // 256):
                gemm_fm(Wl, KC, cfg.off["c_val"] + i * 256, 256, rhs_xn, [xnB], NTB, 512, epi_val)
                gemm_fm(Wl, KC, cfg.off["c_glu"] + i * 256, 256, rhs_xn, [xnB], NTB, 512, epi_glu)
            def mk_simple(base, func, dst, scale=None):
                def epi(col, m, tb, pp, ppB):
                    r0 = col - base
                    s1, s1B = nstg()
                    act(s1[0:m, :], pp[0][0:m, :], func, [ppB[0]], [s1B])
                    if scale is not None:
                        dve(lambda e: e.tensor_scalar(out=s1[0:m, :], in0=s1[0:m, :], scalar1=scale, scalar2=None, op0=ALU.mult), [s1B], [s1B])
                    store(dst[r0:r0 + m, tb * 512:(tb + 1) * 512], s1[0:m, :], [s1B])
                return epi
            gemm_fm(Wl, KC, cfg.off["c_gate"], CW, rhs_xn, [xnB], NTB, 512, mk_simple(cfg.off["c_gate"], AF.Silu, gate_d[0]))
            gemm_fm(Wl, KC, cfg.off["a_gate"], CW, rhs_xn, [xnB], NTB, 512, mk_simple(cfg.off["a_gate"], AF.Silu, gate_d[1]))
            gemm_fm(Wl, KC, cfg.off["r_gate"], HW, rhs_xn, [xnB], NTB, 512, mk_simple(cfg.off["r_gate"], AF.Silu, gate_d[2]))
            gemm_fm(Wl, KC, cfg.off["a_q"], KQ, rhs_xn, [xnB], NTB, 512, mk_simple(cfg.off["a_q"], AF.Copy, gq_d, scale=float(cfg.GDK) ** -0.5))
            gemm_fm(Wl, KC, cfg.off["a_k"], KQ, rhs_xn, [xnB], NTB, 512, mk_simple(cfg.off["a_k"], AF.Copy, gk_d))
            gemm_fm(Wl, KC, cfg.off["r_q"], HW, rhs_xn, [xnB], NTB, 512, mk_simple(cfg.off["r_q"], AF.Silu, hq_d, scale=128.0 ** -0.5))
            for mi in range(3):
                gemm_fm(Wl, KC, cfg.off["merge"] + mi * D, D, rhs_xn, [xnB], NTB, 512, mk_simple(cfg.off["merge"] + mi * D, AF.Sigmoid, merge_d[mi]))
            def epi_lr(col, m, tb, pp, ppB):
                t, tB_ = nbst()
                act(t[0:16, :], pp[0][0:16, :], AF.Copy, [ppB[0]], [tB_])
                store(alr_d[:, tb * 512:(tb + 1) * 512], t[0:16, :], [tB_])
            gemm_fm(Wl, KC, cfg.off["a_lr"], 16, rhs_xn, [xnB], NTB, 512, epi_lr)
            def epi_rf(col, m, tb, pp, ppB):
                f = (col - cfg.off["r_f"]) // 128
                s1, s1B = nstg()
                act(s1[:], pp[0][:, :], AF.Sigmoid, [ppB[0]], [s1B])
                dve(lambda e: e.tensor_scalar(out=s1[:], in0=s1[:], scalar1=omlb[:, f:f + 1], scalar2=lbv[:, f:f + 1], op0=ALU.mult, op1=ALU.add), [s1B, lbB], [s1B])
                s2, s2B = nstg()
                dve(lambda e: e.tensor_scalar(out=s2[:], in0=s1[:], scalar1=-1.0, scalar2=1.0, op0=ALU.mult, op1=ALU.add), [s1B], [s2B])
                store(hk_d[f * 128:(f + 1) * 128, tb * 512:(tb + 1) * 512], s2[:], [s2B])
                s3, s3B = nstg()
                act(s3[:], s1[:], AF.Ln, [s1B], [s3B])
                store(hg_d[f * 128:(f + 1) * 128, tb * 512:(tb + 1) * 512], s3[:], [s3B])
            gemm_fm(Wl, KC, cfg.off["r_f"], HW, rhs_xn, [xnB], NTB, 512, epi_rf)
            def mk_tm(base, dst):
                def epi(col, cb, tt, pp, ppB):
                    t, tB_ = nbst()
                    act(t[:, 0:cb], pp[:, 0:cb], AF.Copy, [ppB], [tB_])
                    store(dst[tt * 128:(tt + 1) * 128, col - base:col - base + cb], t[:, 0:cb], [tB_])
                return epi
            lhs_xn = lambda k, tt: xn[:, k, tt * 128:(tt + 1) * 128]
            gemm_tm(Wl, KC, cfg.off["a_v"], CW, lhs_xn, [xnB], TB // 128, mk_tm(cfg.off["a_v"], gv_d))
            gemm_tm(Wl, KC, cfg.off["r_i"], HW, lhs_xn, [xnB], TB // 128, mk_tm(cfg.off["r_i"], hv_d))
            tr.barrier()
            arena.reset()
            alr = arena.alloc(TB, BF16); alrB = Buf()
            dve(lambda e: e.memset(alr, 0.0), [], [alrB])
            load(alr[0:16, :], alr_d[:, :], [alrB])
            def epi_g(col, m, tb, pp, ppB):
                f = col // 128
                s1, s1B = nstg()
                act(s1[:], pp[0][:, :], AF.Exp, [ppB[0], lbB], [s1B], bias=negba[:, f:f + 1], scale=-1.0)
                act(s1[:], s1[:], AF.Ln, [s1B], [s1B], bias=1.0)
                dve(lambda e: e.tensor_scalar(out=s1[:], in0=s1[:], scalar1=-1.0 / 16.0, scalar2=None, op0=ALU.mult), [s1B], [s1B])
                store(gg_d[col:col + 128, tb * 512:(tb + 1) * 512], s1[:], [s1B])
            gemm_fm(wa2[l], 1, 0, KQ, lambda k, tb: alr[:, tb * 512:(tb + 1) * 512], [alrB], NTB, 512, epi_g, cbmax=512)
            tr.barrier()
            linattn(cfg.GH, cfg.GDK, cfg.GDV, 64, gq_d, gk_d, gg_d, gv_d, gate_d[1], ("gng", l), stg_d, y_d[CW:2 * CW], maskg, rstg, sbi == 0)
            linattn(cfg.HH, 128, 128, 32, hq_d, hk_d, hg_d, hv_d, gate_d[2], ("hng", l), sth_d, y_d[2 * CW:3 * CW], maskh, rsth, sbi == 0)
            arena.reset()
            NF = CW // 128
            cvo = arena.alloc(NF * TB, F32).rearrange("p (f t) -> p f t", t=TB); cvB = [Buf() for _ in range(NF)]
            lnb = arena.alloc(NF * TB, BF16).rearrange("p (f t) -> p f t", t=TB); lnB = Buf()
            st_sum = [(ps[0], psB[0]), (ps[1], psB[1])]; st_sq = [(ps[2], psB[2]), (ps[3], psB[3])]
            for f in range(NF):
                ae = aexts[f % 2]; aeB = aextB[f % 2]
                load(ae[:, 0:TB + 30], aT_d[f * 128:(f + 1) * 128, AOFF - 30 + t0:AOFF + t0 + TB], [aeB])
                def cv(e, f=f, ae=ae):
                    o = cvo[:, f, :]
                    w0 = cfg.vo[("cw", l)] + f * 31
                    e.tensor_scalar(out=o, in0=ae[:, 30:30 + TB], scalar1=vecs[:, w0 + 30:w0 + 31], scalar2=vcol(("cb", l), f), op0=ALU.mult, op1=ALU.add)
                    for k in range(30):
                        ins = e.scalar_tensor_tensor(out=o, in0=ae[:, k:k + TB], scalar=vecs[:, w0 + k:w0 + k + 1], in1=o, op0=ALU.mult, op1=ALU.add)
                    return ins
                dve(cv, [aeB, vecsB], [cvB[f]])
                for tb in range(NTB):
                    sl = slice(tb * 512, (tb + 1) * 512)
                    xb, xbB = nbst()
                    act(xb[:], cvo[:, f, sl], AF.Copy, [cvB[f]], [xbB])
                    tr.emit("pe", lambda e, f=f, tb=tb, xb=xb: e.matmul(st_sum[tb][0][:, :], lhsT=ones[:], rhs=xb[:], start=(f == 0), stop=(f == NF - 1)), reads=[xbB, constB], writes=[st_sum[tb][1]])
                    sq, sqB = nbst()
                    act(sq[:], cvo[:, f, sl], AF.Square, [cvB[f]], [sqB])
                    tr.emit("pe", lambda e, f=f, tb=tb, sq=sq: e.matmul(st_sq[tb][0][:, :], lhsT=ones[:], rhs=sq[:], start=(f == 0), stop=(f == NF - 1)), reads=[sqB, constB], writes=[st_sq[tb][1]])
            for tb in range(NTB):
                sl = slice(tb * 512, (tb + 1) * 512)
                mu, muB = keep[1], keepB[1]; rs, rsB = keep[0], keepB[0]
                act(mu[:], st_sum[tb][0][:, :], AF.Copy, [st_sum[tb][1]], [muB], scale=1.0 / CW)
                def vf(e, mu=mu, rs=rs, tb=tb):
                    e.tensor_tensor(out=rs[:], in0=mu[:], in1=mu[:], op=ALU.mult)
                    return e.scalar_tensor_tensor(out=rs[:], in0=st_sq[tb][0][:, :], scalar=1.0 / CW, in1=rs[:], op0=ALU.mult, op1=ALU.subtract)
                dve(vf, [muB, st_sq[tb][1]], [rsB])
                act(rs[:], rs[:], AF.Sqrt, [rsB], [rsB], bias=EPS)
                dve(lambda e, rs=rs: e.reciprocal(out=rs[:], in_=rs[:]), [rsB], [rsB])
                for f in range(NF):
                    t1, t1B = nstg()
                    def lf(e, f=f, t1=t1, mu=mu, rs=rs, sl=sl):
                        e.tensor_tensor(out=t1[:], in0=cvo[:, f, sl], in1=mu[:], op=ALU.subtract)
                        return e.tensor_tensor(out=t1[:], in0=t1[:], in1=rs[:], op=ALU.mult)
                    dve(lf, [cvB[f], muB, rsB], [t1B])
                    act(lnb[:, f, sl], t1[:], AF.Silu, [t1B, vecsB], [lnB], bias=vcol(("lnb", l), f), scale=vcol(("lng", l), f))
            def epi_pw(col, m, tb, pp, ppB):
                gt, gtB = nstg()
                load(gt[:], gate_d[0][col:col + 128, tb * 512:(tb + 1) * 512], [gtB])
                yb, ybB = nbst()
                dve(lambda e: e.tensor_tensor(out=yb[:], in0=pp[0][:, :], in1=gt[:], op=ALU.mult), [ppB[0], gtB], [ybB])
                store(y_d[col:col + 128, tb * 512:(tb + 1) * 512], yb[:], [ybB])
            st["p"] = 4
            gemm_fm(conv_pw[l], NF, 0, CW, lambda k, tb: lnb[:, k, tb * 512:(tb + 1) * 512], [lnB], NTB, 512, epi_pw)
            tr.barrier()
            arena.reset()
            K3 = 3 * NF
            yv = [arena.alloc(NF * TB, BF16).rearrange("p (f t) -> p f t", t=TB) for _ in range(3)]; yvB = Buf()
            for b in range(3):
                load(yv[b], y_d[b * CW:(b + 1) * CW, :].rearrange("(f p) t -> p f t", p=128), [yvB])
            def epi_mg(col, m, tb, pp, ppB):
                sl = slice(tb * 512, (tb + 1) * 512)
                gts = []
                for b in range(3):
                    gt, gtB = nstg()
                    load(gt[:], merge_d[b][col:col + 128, sl], [gtB]); gts.append((gt, gtB))
                def mf(e):
                    e.tensor_tensor(out=gts[0][0][:], in0=pp[0][:, :], in1=gts[0][0][:], op=ALU.mult)
                    e.tensor_tensor(out=gts[1][0][:], in0=pp[1][:, :], in1=gts[1][0][:], op=ALU.mult)
                    e.tensor_tensor(out=gts[2][0][:], in0=pp[2][:, :], in1=gts[2][0][:], op=ALU.mult)
                    e.tensor_tensor(out=gts[0][0][:], in0=gts[0][0][:], in1=gts[1][0][:], op=ALU.add)
                    return e.tensor_tensor(out=gts[0][0][:], in0=gts[0][0][:], in1=gts[2][0][:], op=ALU.add)
                dve(mf, list(ppB) + [g[1] for g in gts], [g[1] for g in gts])
                ub, ubB = nbst()
                act(ub[:], gts[0][0][:], AF.Copy, [gts[0][1]], [ubB])
                store(u_d[col:col + 128, sl], ub[:], [ubB])
            st["p"] = 0
            gemm_fm(w_branch[l], K3, 0, D, lambda k, tb: yv[k // NF][:, k % NF, tb * 512:(tb + 1) * 512], [yvB], NTB, 512, epi_mg, cbmax=128, groups=3)
            tr.barrier()
            arena.reset()
            uv = arena.alloc(KC * TB, BF16).rearrange("p (c t) -> p c t", t=TB); uvB = Buf()
            load(uv, u_d.rearrange("(c p) t -> p c t", p=128), [uvB])
            def mk_res(src):
                def epi(col, m, tb, pp, ppB):
                    sl = slice(t0 + tb * 512, t0 + (tb + 1) * 512)
                    ht, htB = nstg()
                    load(ht[:], src[col:col + 128, sl], [htB])
                    dve(lambda e: e.tensor_tensor(out=ht[:], in0=pp[0][:, :], in1=ht[:], op=ALU.add), [ppB[0], htB], [htB])
                    store(hT_d[col:col + 128, sl], ht[:], [htB])
                return epi
            gemm_fm(w_out[l], KC, 0, D, lambda k, tb: uv[:, k, tb * 512:(tb + 1) * 512], [uvB], NTB, 512, mk_res(hsrc))
            tr.barrier()
            arena.reset()
            hn = arena.alloc(KC * TB, BF16).rearrange("p (c t) -> p c t", t=TB); hnB = Buf()
            norm(hT_d[:, t0:t0 + TB], TB, ("n2g", l), out_sb=hn, out_sbB=hnB)
            NX = XD // 128
            qx = arena.alloc(NX * TB, BF16).rearrange("p (c t) -> p c t", t=TB); qxB = Buf()
            kxs = arena.alloc(NX * M, BF16).rearrange("p (c m) -> p c m", m=M)
            vxs = arena.alloc((M // 128) * XD, BF16).rearrange("p (c d) -> p c d", d=XD); kvB = Buf()
            load(kxs, kx_d.rearrange("(c p) m -> p c m", p=128), [kvB])
            load(vxs, vx_d.rearrange("(c p) d -> p c d", p=128), [kvB])
            def epi_q(col, m, tb, pp, ppB):
                act(qx[:, col // 128, tb * 512:(tb + 1) * 512], pp[0][:, :], AF.Copy, [ppB[0]], [qxB])
            gemm_fm(xq[l], KC, 0, XD, lambda k, tb: hn[:, k, tb * 512:(tb + 1) * 512], [hnB], NTB, 512, epi_q)
            tr.barrier()
            arena.pos[0] = 0
            ox = arena.alloc(NX * TB, BF16).rearrange("p (c t) -> p c t", t=TB); oxB = Buf()
            pts = [arena.alloc(2 * 128, BF16).rearrange("p (c t) -> p c t", t=128) for _ in range(2)]; ptsB = [Buf(), Buf()]
            pbf = [arena.alloc(M, BF16) for _ in range(2)]; pbfB = [Buf(), Buf()]
            NDX = cfg.XHD // 128
            xscale = float(cfg.XHD) ** -0.5
            it = 0
            for hh in range(cfg.XH):
                for tt in range(TB // 128):
                    tsl = slice(tt * 128, (tt + 1) * 128)
                    sc, scB = npsum(0, 3)
                    def ms(e, hh=hh, tsl=tsl, sc=sc):
                        for dc in range(NDX):
                            ins = e.matmul(sc[:, 0:M], lhsT=qx[:, hh * NDX + dc, tsl], rhs=kxs[:, hh * NDX + dc, :], start=(dc == 0), stop=(dc == NDX - 1))
                        return ins
                    tr.emit("pe", ms, reads=[qxB, kvB], writes=[scB])
                    o8 = 32 + (it % 4) * 4
                    mxs = smalls[:, o8:o8 + 1]; sms = smalls[:, o8 + 1:o8 + 2]
                    ef, efB = nstg()
                    dve(lambda e, sc=sc, mxs=mxs: e.tensor_reduce(out=mxs, in_=sc[:, 0:M], axis=AX.X, op=ALU.max), [scB, smallB], [smallB])
                    dve(lambda e, sc=sc, mxs=mxs, ef=ef: e.tensor_tensor(out=ef[:, 0:M], in0=sc[:, 0:M], in1=mxs.to_broadcast([128, M]), op=ALU.subtract), [scB, smallB], [efB])
                    act(ef[:, 0:M], ef[:, 0:M], AF.Exp, [efB], [efB], scale=xscale)
                    pb, pbB = pbf[it % 2], pbfB[it % 2]
                    dve(lambda e, ef=ef, sms=sms: e.tensor_reduce(out=sms, in_=ef[:, 0:M], axis=AX.X, op=ALU.add), [efB, smallB], [smallB])
                    dve(lambda e, sms=sms: e.reciprocal(out=sms, in_=sms), [smallB], [smallB])
                    dve(lambda e, ef=ef, sms=sms, pb=pb: e.tensor_tensor(out=pb, in0=ef[:, 0:M], in1=sms.to_broadcast([128, M]), op=ALU.mult), [efB, smallB], [pbB])
                    def tpf(e, pb=pb):
                        for mc in range(M // 128):
                            ins = e.transpose(out=pT[:, mc * 128:(mc + 1) * 128], in_=pb[:, mc * 128:(mc + 1) * 128], identity=ident[:])
                        return ins
                    tr.emit("pe", tpf, reads=[pbB, constB], writes=[pTB])
                    pt2, pt2B = pts[it % 2], ptsB[it % 2]
                    act(pt2, pT[:, 0:M].rearrange("p (c t) -> p c t", t=128), AF.Copy, [pTB], [pt2B])
                    for dc in range(NDX):
                        po, poB = npsum(3, 7)
                        def mo(e, hh=hh, dc=dc, po=po, pt2=pt2):
                            for mc in range(M // 128):
                                ins = e.matmul(po[:, 0:128], lhsT=vxs[:, mc, hh * cfg.XHD + dc * 128:hh * cfg.XHD + (dc + 1) * 128], rhs=pt2[:, mc, :], start=(mc == 0), stop=(mc == M // 128 - 1))
                            return ins
                        tr.emit("pe", mo, reads=[kvB, pt2B], writes=[poB])
                        act(ox[:, hh * NDX + dc, tsl], po[:, 0:128], AF.Copy, [poB], [oxB])
                    it += 1
            if "ox_dbg" in dbg:
                for k in range(NX):
                    store(qx_dbg[k * 128:(k + 1) * 128, :], qx[:, k, :], [qxB])
                    store(ox_dbg[k * 128:(k + 1) * 128, :], ox[:, k, :], [oxB])
            st["p"] = 0
            gemm_fm(xo[l], NX, 0, D, lambda k, tb: ox[:, k, tb * 512:(tb + 1) * 512], [oxB], NTB, 512, mk_res(hT_d))
            tr.barrier()

    for sbi in range(cfg.NSB):
        norm(hT_d[:, sbi * TB:(sbi + 1) * TB], TB, "fng", out_d=outT[:, sbi * TB:(sbi + 1) * TB], out_dt=F32)
    tr.barrier()

    sems = {}
    for i, k in enumerate(sorted(tr.keys(), key=str)):
        sems[k] = es.enter_context(nc.semaphore(f"s{i}"))
    def replay(eng_name, e):
        known = {}
        for waits, fn, inc in tr.ops[eng_name]:
            for (k, v) in waits:
                if known.get(k, 0) >= v: continue
                e.wait_ge(sems[k], v); known[k] = v
            ins = fn(e)
            ins.then_inc(sems[inc[0]], inc[1])
        for (k, v) in tr.pending[eng_name]:
            if known.get(k, 0) >= v: continue
            e.wait_ge(sems[k], v); known[k] = v
    with nc.Block() as block:
        @block.tensor
        def _(e): replay("pe", e)
        @block.scalar
        def _(e): replay("act", e)
        @block.vector
        def _(e): replay("dve", e)
        @block.gpsimd
        def _(e): replay("pool", e)
        @block.sync
        def _(e): replay("sp", e)
    es.close()
    return nc


def prep_inputs(cfg, I, lay, hT):
    f = lambda a: np.ascontiguousarray(np.asarray(a, dtype=np.float32))
    sl = slice(lay, lay + 1)
    m = {"xT": hT, "memT": f(np.asarray(I["mem"])[0].T), "w_in": f(I["w_in"][sl]),
         "conv_pw": f(I["conv_pw"][sl]), "gla_wa2": f(I["gla_wa2"][sl]),
         "w_branch": f(np.asarray(I["w_branch"][sl]).reshape(1, 3 * cfg.CW, cfg.D)), "w_out": f(I["w_out"][sl]),
         "xq": f(I["xq"][sl]), "xkv": f(I["xkv"][sl]), "xo": f(I["xo"][sl]),
         "vecs": pack_vecs(cfg, I, lay)}
    m.update(make_consts(cfg))
    return m


def run(cfg, I, stop_after=None, dbg=(), nlayers=None):
    I = {k: np.asarray(v) for k, v in I.items()}
    nc = build(cfg, stop_after=stop_after, dbg=dbg)
    hT = np.ascontiguousarray(I["x"][0].T.astype(np.float32))
    res = None
    for lay in range(cfg.LT if nlayers is None else nlayers):
        m = prep_inputs(cfg, I, lay, hT)
        res = run_bass_kernel_spmd(nc, [m], core_ids=[0]).results[0]
        hT = np.ascontiguousarray(res["hT_d"])
    out = np.ascontiguousarray(res["outT"].T)[None]
    return (out, res) if dbg else out


def kernel(**inputs):
    cfg = Cfg()
    return run(cfg, inputs).astype(np.float32)
```
